# Optimizing a Trainium2 kernel written in Bass

```python
import math
import jax, jax.numpy as jnp
from jax import lax
import numpy as np

D_MODEL = 2048
BATCH = 4
SEQ = 8192
DEPTH = 4

ATTN_HEADS = 8
QK_DIM = 64
V_DIM = 2 * QK_DIM
ATTN_WIDTH = ATTN_HEADS * V_DIM
Q_BLOCK = 128
ROPE_THETA = 10000.0
SSM_WIDTH = D_MODEL // 2
SSM_GROUP = 16
SSM_GROUPS = SSM_WIDTH // SSM_GROUP
SSM_STATE = 64
SSM_CHUNK = 128
D_FF = 4 * D_MODEL
LN_EPS = 1e-5
RMS_EPS = 1e-5
DEEPNORM_ALPHA = (2.0 * DEPTH) ** 0.25
DEEPNORM_BETA = (8.0 * DEPTH) ** -0.25

Q_COLS = ATTN_HEADS * 2 * QK_DIM
K_COLS = ATTN_HEADS * 2 * QK_DIM
V_COLS = ATTN_WIDTH
U_COLS = SSM_WIDTH
G_COLS = D_MODEL
OFF_K = Q_COLS
OFF_V = OFF_K + K_COLS
OFF_U = OFF_V + V_COLS
OFF_GA = OFF_U + U_COLS
OFF_GS = OFF_GA + G_COLS
IN_COLS = OFF_GS + G_COLS

kernel_name = "gated_diffattn_s5_hybrid_deepnorm"


def _lambda_init(layer):
    return 0.8 - 0.6 * math.exp(-0.3 * layer)


def layer_norm(x, g, b):
    xf = x.astype(jnp.float32)
    mu = jnp.mean(xf, axis=-1, keepdims=True)
    xc = xf - mu
    var = jnp.mean(jnp.square(xc), axis=-1, keepdims=True)
    y = xc * lax.rsqrt(var + LN_EPS) * g.astype(jnp.float32) + b.astype(jnp.float32)
    return y.astype(x.dtype)


def rope_tables(positions, dtype):
    inv = ROPE_THETA ** (-jnp.arange(0, QK_DIM, 2, dtype=jnp.float32) / QK_DIM)
    ang = positions.astype(jnp.float32)[..., None] * inv
    return jnp.cos(ang).astype(dtype), jnp.sin(ang).astype(dtype)


def apply_rope(t, cos, sin):
    c = cos[:, :, None, None, :]
    s = sin[:, :, None, None, :]
    half = QK_DIM // 2
    t1, t2 = t[..., :half], t[..., half:]
    return jnp.concatenate([t1 * c - t2 * s, t2 * c + t1 * s], axis=-1)


def diff_attention(q, k, v, lam, lam_init, subln_g):
    bsz, seq = q.shape[0], q.shape[1]
    n_blocks = seq // Q_BLOCK
    scale = QK_DIM ** -0.5
    kpos = jnp.arange(seq)

    def block(i):
        start = i * Q_BLOCK
        qb = lax.dynamic_slice_in_dim(q, start, Q_BLOCK, axis=1)
        s = jnp.einsum('bqhcd,bkhcd->bhcqk', qb, k).astype(jnp.float32) * scale
        qpos = start + jnp.arange(Q_BLOCK)
        mask = kpos[None, :] <= qpos[:, None]
        p = jax.nn.softmax(jnp.where(mask, s, -jnp.inf), axis=-1)
        w = p[:, :, 0] - lam * p[:, :, 1]
        return jnp.einsum('bhqk,bkhe->bqhe', w.astype(v.dtype), v)

    out = lax.map(block, jnp.arange(n_blocks))
    out = jnp.moveaxis(out, 0, 1).reshape(bsz, seq, ATTN_HEADS, V_DIM)
    of = out.astype(jnp.float32)
    of = of * lax.rsqrt(jnp.mean(jnp.square(of), axis=-1, keepdims=True) + RMS_EPS)
    of = of * subln_g.astype(jnp.float32) * (1.0 - lam_init)
    return of.astype(v.dtype).reshape(bsz, seq, ATTN_WIDTH)


def _complex_affine_combine(e1, e2):
    a1r, a1i, b1r, b1i = e1
    a2r, a2i, b2r, b2i = e2
    ar = a2r * a1r - a2i * a1i
    ai = a2r * a1i + a2i * a1r
    br = a2r * b1r - a2i * b1i + b2r
    bi = a2r * b1i + a2i * b1r + b2i
    return (ar, ai, br, bi)


def s5_ssm(u, a_re, a_im, log_dt, b_re, b_im, c_re, c_im, d_skip, w_glu):
    out_dtype = u.dtype
    f32 = jnp.float32
    uf = u.astype(f32)
    a_re = a_re.astype(f32); a_im = a_im.astype(f32)
    b_re = b_re.astype(f32); b_im = b_im.astype(f32)
    c_re = c_re.astype(f32); c_im = c_im.astype(f32)
    dt = jnp.exp(log_dt.astype(f32))[:, None]
    mag = jnp.exp(dt * a_re)
    ab_re = mag * jnp.cos(dt * a_im)
    ab_im = mag * jnp.sin(dt * a_im)
    den = jnp.square(a_re) + jnp.square(a_im)
    nr, ni = ab_re - 1.0, ab_im
    z_re = (nr * a_re + ni * a_im) / den
    z_im = (ni * a_re - nr * a_im) / den
    bb_re = z_re[..., None] * b_re - z_im[..., None] * b_im
    bb_im = z_re[..., None] * b_im + z_im[..., None] * b_re

    bsz, seq = uf.shape[0], uf.shape[1]
    n_chunks = seq // SSM_CHUNK
    uc = jnp.moveaxis(uf.reshape(bsz, n_chunks, SSM_CHUNK, SSM_GROUPS, SSM_GROUP), 1, 0)
    el_shape = (bsz, SSM_CHUNK, SSM_GROUPS, SSM_STATE)
    a_el_re = jnp.broadcast_to(ab_re, el_shape)
    a_el_im = jnp.broadcast_to(ab_im, el_shape)

    def step(carry, u_chunk):
        h_re, h_im = carry
        bu_re = jnp.einsum('blgh,gph->blgp', u_chunk, bb_re)
        bu_im = jnp.einsum('blgh,gph->blgp', u_chunk, bb_im)
        acum_re, acum_im, s_re, s_im = lax.associative_scan(
            _complex_affine_combine, (a_el_re, a_el_im, bu_re, bu_im), axis=1)
        x_re = s_re + acum_re * h_re[:, None] - acum_im * h_im[:, None]
        x_im = s_im + acum_re * h_im[:, None] + acum_im * h_re[:, None]
        y = (jnp.einsum('blgp,ghp->blgh', x_re, c_re)
             - jnp.einsum('blgp,ghp->blgh', x_im, c_im))
        return (x_re[:, -1], x_im[:, -1]), y

    init = (jnp.zeros((bsz, SSM_GROUPS, SSM_STATE), f32),
            jnp.zeros((bsz, SSM_GROUPS, SSM_STATE), f32))
    _, ys = lax.scan(step, init, uc)
    y = jnp.moveaxis(ys, 0, 1).reshape(bsz, seq, SSM_GROUPS, SSM_GROUP)
    y = y + d_skip.astype(f32) * uf
    g = jax.nn.gelu(y).reshape(bsz, seq, SSM_WIDTH)
    g = g * jax.nn.sigmoid(g @ w_glu.astype(f32))
    return g.astype(out_dtype)


def hybrid_mixer(x, cos, sin, w_in, lam_qk, lam_init, subln_g,
                 a_re, a_im, log_dt, b_re, b_im, c_re, c_im, d_skip, w_glu,
                 w_attn_up, w_ssm_up, w_out):
    bsz, seq, _ = x.shape
    q = (x @ w_in[:, :OFF_K]).reshape(bsz, seq, ATTN_HEADS, 2, QK_DIM)
    k = (x @ w_in[:, OFF_K:OFF_V]).reshape(bsz, seq, ATTN_HEADS, 2, QK_DIM)
    v = (x @ w_in[:, OFF_V:OFF_U]).reshape(bsz, seq, ATTN_HEADS, V_DIM)
    u = (x @ w_in[:, OFF_U:OFF_GA]).reshape(bsz, seq, SSM_GROUPS, SSM_GROUP)
    gate_a = jax.nn.sigmoid(x @ w_in[:, OFF_GA:OFF_GS])
    gate_s = jax.nn.sigmoid(x @ w_in[:, OFF_GS:])
    q = apply_rope(q, cos, sin)
    k = apply_rope(k, cos, sin)
    lq = lam_qk.astype(jnp.float32)
    lam = (jnp.exp(jnp.sum(lq[0] * lq[1])) - jnp.exp(jnp.sum(lq[2] * lq[3])) + lam_init)
    attn = diff_attention(q, k, v, lam, lam_init, subln_g)
    ssm = s5_ssm(u, a_re, a_im, log_dt, b_re, b_im, c_re, c_im, d_skip, w_glu)
    merged = gate_a * (attn @ w_attn_up) + gate_s * (ssm @ w_ssm_up)
    return merged @ w_out


def setup_inputs(seed: int = 0) -> dict:
    key = jax.random.key(seed)
    ks = jax.random.split(key, 24)
    f32 = jnp.float32
    nrm = lambda k, shape, std: jax.random.normal(k, shape, f32) * std
    x = jax.random.normal(ks[0], (BATCH, SEQ, D_MODEL), f32)
    positions = jnp.broadcast_to(jnp.arange(SEQ, dtype=jnp.int32)[None, :], (BATCH, SEQ))
    w_in = nrm(ks[1], (DEPTH, D_MODEL, IN_COLS), D_MODEL ** -0.5)
    lambda_qk = nrm(ks[2], (DEPTH, 4, QK_DIM), 0.1)
    subln_g = 1.0 + nrm(ks[3], (DEPTH, V_DIM), 0.02)
    ssm_a_re = -0.5 + nrm(ks[4], (DEPTH, SSM_GROUPS, SSM_STATE), 0.01)
    ssm_a_im = jnp.broadcast_to(jnp.pi * jnp.arange(SSM_STATE, dtype=f32), (DEPTH, SSM_GROUPS, SSM_STATE))
    ssm_log_dt = jax.random.uniform(ks[5], (DEPTH, SSM_GROUPS), f32, math.log(1e-3), math.log(1e-1))
    ssm_b_re = nrm(ks[6], (DEPTH, SSM_GROUPS, SSM_STATE, SSM_GROUP), (2.0 * SSM_GROUP) ** -0.5)
    ssm_b_im = nrm(ks[7], (DEPTH, SSM_GROUPS, SSM_STATE, SSM_GROUP), (2.0 * SSM_GROUP) ** -0.5)
    ssm_c_re = nrm(ks[8], (DEPTH, SSM_GROUPS, SSM_GROUP, SSM_STATE), 0.5 ** 0.5)
    ssm_c_im = nrm(ks[9], (DEPTH, SSM_GROUPS, SSM_GROUP, SSM_STATE), 0.5 ** 0.5)
    ssm_d = nrm(ks[10], (DEPTH, SSM_GROUPS, SSM_GROUP), 1.0)
    w_glu = nrm(ks[11], (DEPTH, SSM_WIDTH, SSM_WIDTH), SSM_WIDTH ** -0.5)
    w_attn_up = nrm(ks[12], (DEPTH, ATTN_WIDTH, D_MODEL), ATTN_WIDTH ** -0.5)
    w_ssm_up = nrm(ks[13], (DEPTH, SSM_WIDTH, D_MODEL), SSM_WIDTH ** -0.5)
    w_out = nrm(ks[14], (DEPTH, D_MODEL, D_MODEL), D_MODEL ** -0.5 * DEEPNORM_BETA)
    ln1_g = 1.0 + nrm(ks[15], (DEPTH, D_MODEL), 0.02)
    ln1_b = nrm(ks[16], (DEPTH, D_MODEL), 0.02)
    ln2_g = 1.0 + nrm(ks[17], (DEPTH, D_MODEL), 0.02)
    ln2_b = nrm(ks[18], (DEPTH, D_MODEL), 0.02)
    w_mlp_up = nrm(ks[19], (DEPTH, D_MODEL, D_FF), D_MODEL ** -0.5)
    w_mlp_down = nrm(ks[20], (DEPTH, D_FF, D_MODEL), D_FF ** -0.5 * DEEPNORM_BETA)
    return {"x": x, "positions": positions, "w_in": w_in, "lambda_qk": lambda_qk,
            "subln_g": subln_g, "ssm_a_re": ssm_a_re, "ssm_a_im": ssm_a_im,
            "ssm_log_dt": ssm_log_dt, "ssm_b_re": ssm_b_re, "ssm_b_im": ssm_b_im,
            "ssm_c_re": ssm_c_re, "ssm_c_im": ssm_c_im, "ssm_d": ssm_d, "w_glu": w_glu,
            "w_attn_up": w_attn_up, "w_ssm_up": w_ssm_up, "w_out": w_out,
            "ln1_g": ln1_g, "ln1_b": ln1_b, "ln2_g": ln2_g, "ln2_b": ln2_b,
            "w_mlp_up": w_mlp_up, "w_mlp_down": w_mlp_down}


def reference(x, positions, w_in, lambda_qk, subln_g, ssm_a_re, ssm_a_im, ssm_log_dt,
              ssm_b_re, ssm_b_im, ssm_c_re, ssm_c_im, ssm_d, w_glu, w_attn_up, w_ssm_up,
              w_out, ln1_g, ln1_b, ln2_g, ln2_b, w_mlp_up, w_mlp_down):
    cos, sin = rope_tables(positions, x.dtype)
    for l in range(DEPTH):
        lam_init = _lambda_init(l)
        mix = hybrid_mixer(x, cos, sin, w_in[l], lambda_qk[l], lam_init, subln_g[l],
                           ssm_a_re[l], ssm_a_im[l], ssm_log_dt[l], ssm_b_re[l], ssm_b_im[l],
                           ssm_c_re[l], ssm_c_im[l], ssm_d[l], w_glu[l],
                           w_attn_up[l], w_ssm_up[l], w_out[l])
        x = layer_norm(DEEPNORM_ALPHA * x + mix, ln1_g[l], ln1_b[l])
        h = jnp.square(jax.nn.relu(x @ w_mlp_up[l])) @ w_mlp_down[l]
        x = layer_norm(DEEPNORM_ALPHA * x + h, ln2_g[l], ln2_b[l])
    return x
```

```python
import math
from contextlib import ExitStack

import numpy as np
import concourse.bass as bass
import concourse.mybir as mybir
from concourse.bass_utils import run_bass_kernel_spmd

F32 = mybir.dt.float32
BF16 = mybir.dt.bfloat16
I32 = mybir.dt.int32
AF = mybir.ActivationFunctionType
ALU = mybir.AluOpType

D = 2048
DEPTH = 4
SEQ = 8192
BATCH = 4
NH = 8
LN_EPS = 1e-5
RMS_EPS = 1e-5
ALPHA = (2.0 * DEPTH) ** 0.25
TWO_PI = 6.283185307179586
CW1 = 6.28125
CW2 = 0.0019353071795864769
MAGIC = 12582912.0
PI_LO = 3.1415925
GELU_C = 1.5957691216057308
ENG = ["sync", "scalar", "vector", "gpsimd", "tensor"]
ARENA_KB = 192

C_ID, C_PROT, C_TRI, C_SSM, C_INV, C_SGA, C_SGB, C_BONES, C_IOTA = 0, 128, 256, 384, 512, 513, 514, 528, 544
CW = C_IOTA + 1032


def _lambda_init(layer):
    return 0.8 - 0.6 * math.exp(-0.3 * layer)


def host_consts():
    c = np.zeros((128, CW), np.float32)
    c[:, C_ID:C_ID + 128] = np.eye(128, dtype=np.float32)
    for r in range(128):
        if (r % 64) < 32:
            c[r + 32, C_PROT + r] = -1.0
        else:
            c[r - 32, C_PROT + r] = 1.0
    k = np.arange(128)
    c[:, C_TRI:C_TRI + 128] = (k[:, None] <= k[None, :]).astype(np.float32)
    m = k // 16
    c[:, C_SSM:C_SSM + 128] = (m[None, :] >= m[:, None]).astype(np.float32)
    inv = (10000.0 ** (-np.arange(0, 64, 2, dtype=np.float32) / 64)).astype(np.float32)
    c[:, C_INV] = inv[k % 32]
    c[:, C_SGA] = np.where(k < 64, 1.0, -1.0)
    c[:, C_SGB] = np.where(k < 64, -1.0, 1.0)
    c[:, C_BONES] = (k < 64).astype(np.float32)
    c[:, C_BONES + 1] = (k >= 64).astype(np.float32)
    c[:, C_IOTA:C_IOTA + 1032] = np.arange(1032, dtype=np.float32)[None, :]
    return c


class Sched:
    def __init__(self, sem_alloc):
        self.sem_alloc = sem_alloc
        self.sems = {}
        self.tot = {}
        self.seen = {n: {} for n in ENG}
        self.res = {}
        self.prog = {n: [] for n in ENG}
        self.n_ins = 0
        for n in ENG:
            self._sem("E_" + n)

    def _sem(self, name):
        if name not in self.sems:
            self.sems[name] = self.sem_alloc(name)
            self.tot[name] = 0
        return self.sems[name]

    def _wait(self, eng, tok):
        if tok is None:
            return
        s, v = tok
        if self.seen[eng].get(s, 0) >= v:
            return
        sem = self.sems[s]
        self.prog[eng].append(lambda e, sem=sem, v=v: e.wait_ge(sem, v))
        self.seen[eng][s] = v
        self.n_ins += 1

    def _deps(self, eng, reads, writes):
        own = "E_" + eng
        pe = eng == "tensor"
        for k in reads:
            r = self.res.get(k)
            if r is not None and not (pe and r[0] is not None and r[0][0] == own):
                self._wait(eng, r[0])
        for k in writes:
            r = self.res.get(k)
            if r is not None:
                if not (pe and r[0] is not None and r[0][0] == own):
                    self._wait(eng, r[0])
                for s, v in r[1].items():
                    if s != own:
                        self._wait(eng, (s, v))

    def _commit(self, tok, reads, writes):
        s, v = tok
        for k in reads:
            r = self.res.setdefault(k, [None, {}])
            if r[1].get(s, 0) < v:
                r[1][s] = v
        for k in writes:
            self.res[k] = [tok, {}]

    def op(self, eng, fn, reads=(), writes=()):
        self._deps(eng, reads, writes)
        s = "E_" + eng
        self.tot[s] += 1
        sem = self.sems[s]
        self.prog[eng].append(lambda e, fn=fn, sem=sem: fn(e).then_inc(sem, 1))
        tok = (s, self.tot[s])
        self._commit(tok, reads, writes)
        self.n_ins += 1
        return tok

    def group(self, eng, fns, reads=(), writes=()):
        self._deps(eng, reads, writes)
        fns = list(fns)
        for fn in fns[:-1]:
            self.prog[eng].append(fn)
        s = "E_" + eng
        self.tot[s] += 1
        sem = self.sems[s]
        self.prog[eng].append(lambda e, fn=fns[-1], sem=sem: fn(e).then_inc(sem, 1))
        self.n_ins += len(fns)
        tok = (s, self.tot[s])
        self._commit(tok, reads, writes)
        return tok

    def dma(self, q, out, in_, reads=(), writes=(), sem=None):
        self._deps(q, reads, writes)
        sname = "D_" + (sem if sem is not None else (writes[0] if writes else reads[0]))
        self._sem(sname)
        self.tot[sname] += 16
        sh = self.sems[sname]
        self.prog[q].append(lambda e, out=out, in_=in_, sh=sh: e.dma_start(out=out, in_=in_).then_inc(sh, 16))
        tok = (sname, self.tot[sname])
        self._commit(tok, reads, writes)
        self.n_ins += 1
        return tok

    def barrier(self):
        for n in ENG:
            for s, v in self.tot.items():
                if v > 0:
                    self._wait(n, (s, v))
        self.res = {}

    def replay(self, nc):
        with nc.Block() as block:
            for n in ENG:
                prog = self.prog[n]
                if not prog:
                    continue

                def body(e, prog=prog):
                    for f in prog:
                        f(e)
                getattr(block, n)(body)
        self.prog = {n: [] for n in ENG}


class Builder:
    def __init__(self, S, NL, dbg=()):
        self.S_tok = S
        self.NL = NL
        self.dbg = set(dbg)
        self.nc = bass.Bass("TRN2", target_bir_lowering=False)
        self.es = ExitStack()

    def din(self, name, shape, dt=F32):
        return self.nc.dram_tensor(name, list(shape), dt, kind="ExternalInput").ap()

    def dscr(self, name, shape, dt=BF16):
        kind = "ExternalOutput" if name in self.dbg else "Internal"
        return self.nc.dram_tensor(name, list(shape), dt, kind=kind).ap()

    def sb(self, name, shape, dt=F32):
        return self.es.enter_context(self.nc.sbuf_tensor(name, list(shape), dt))

    def carve_reset(self):
        self.aoff = 0

    def carve(self, shape, dt=BF16):
        n = int(np.prod(shape))
        units = n if dt == BF16 else 2 * n
        units = (units + 15) // 16 * 16
        assert self.aoff + units <= ARENA_KB * 512, ("arena overflow", self.aoff, units)
        v = self.arena[:, self.aoff:self.aoff + (n if dt == BF16 else 2 * n)]
        self.aoff += units
        if dt != BF16:
            v = v.bitcast(dt)
        if len(shape) == 2:
            v = v.rearrange("p (a b) -> p a b", a=shape[0])
        elif len(shape) == 3:
            v = v.rearrange("p (a b c) -> p a b c", a=shape[0], b=shape[1])
        elif len(shape) == 4:
            v = v.rearrange("p (a b c d) -> p a b c d", a=shape[0], b=shape[1], c=shape[2])
        return v

    def V(self, fn, r=(), w=()):
        return self.S.op("vector", fn, r, w)

    def A(self, fn, r=(), w=()):
        return self.S.op("scalar", fn, r, w)

    def G(self, fn, r=(), w=()):
        return self.S.op("gpsimd", fn, r, w)

    def PE(self, fns, r=(), w=()):
        return self.S.group("tensor", fns, r, w)

    def LD(self, out, in_, r=(), w=(), sem=None):
        return self.S.dma("sync", out, in_, r, w, sem)

    def ST(self, out, in_, r=(), w=(), sem=None):
        return self.S.dma("gpsimd", out, in_, r, w, sem)

    def sin_rr(self, out, ang, tmp, keys, eng="vector"):
        op = self.V if eng == "vector" else self.G
        op(lambda e: e.tensor_scalar(tmp, ang, 1.0 / TWO_PI, MAGIC, ALU.mult, ALU.add), keys, keys)
        op(lambda e: e.tensor_single_scalar(tmp, tmp, MAGIC, ALU.subtract), keys, keys)
        op(lambda e: e.scalar_tensor_tensor(ang, tmp, -CW1, ang, ALU.mult, ALU.add), keys, keys)
        op(lambda e: e.scalar_tensor_tensor(ang, tmp, -CW2, ang, ALU.mult, ALU.add), keys, keys)
        op(lambda e: e.tensor_scalar(ang, ang, PI_LO, -PI_LO, ALU.min, ALU.max), keys, keys)
        self.A(lambda e: e.activation(out, ang, AF.Sin), keys, keys)

    def build(self):
        nc, S, NL = self.nc, self.S_tok, self.NL
        es = self.es
        self.x = self.din("x", [S, D])
        self.pos = self.din("positions", [1, S], I32)
        self.cst_d = self.din("cst", [128, CW])
        L = NL
        self.w_in = self.din("w_in", [L, D, 8192])
        self.lambda_qk = self.din("lambda_qk", [L, 4, 64])
        self.subln_g = self.din("subln_g", [L, 128])
        self.a_re = self.din("ssm_a_re", [L, 64, 64])
        self.a_im = self.din("ssm_a_im", [L, 64, 64])
        self.log_dt = self.din("ssm_log_dt", [L, 64])
        self.b_re = self.din("ssm_b_re", [L, 64, 64, 16])
        self.b_im = self.din("ssm_b_im", [L, 64, 64, 16])
        self.c_re = self.din("ssm_c_re", [L, 64, 16, 64])
        self.c_im = self.din("ssm_c_im", [L, 64, 16, 64])
        self.ssm_d = self.din("ssm_d", [L, 64, 16])
        self.w_glu = self.din("w_glu", [L, 1024, 1024])
        self.w_au = self.din("w_attn_up", [L, 1024, D])
        self.w_su = self.din("w_ssm_up", [L, 1024, D])
        self.w_out = self.din("w_out", [L, D, D])
        self.ln1_g = self.din("ln1_g", [L, D])
        self.ln1_b = self.din("ln1_b", [L, D])
        self.ln2_g = self.din("ln2_g", [L, D])
        self.ln2_b = self.din("ln2_b", [L, D])
        self.w1 = self.din("w_mlp_up", [L, D, 8192])
        self.w2 = self.din("w_mlp_down", [L, 8192, D])
        self.lparams = self.din("lparams", [L, 2])
        self.y = nc.dram_tensor("y", [S, D], F32, kind="ExternalOutput").ap()

        self.Win_b = self.dscr("Win_b", [L, 16, 128, 4, 16, 128])
        self.W1_b = self.dscr("W1_b", [L, 16, 128, 4, 16, 128])
        self.Wglu_b = self.dscr("Wglu_b", [L, 1, 128, 8, 8, 128])
        self.Wau_b = self.dscr("Wau_b", [L, 2, 128, 8, 8, 128])
        self.Wsu_b = self.dscr("Wsu_b", [L, 2, 128, 8, 8, 128])
        self.Wout_b = self.dscr("Wout_b", [L, 4, 128, 16, 512])
        self.W2_b = self.dscr("W2_b", [L, 4, 4, 128, 16, 512])
        self.QT = self.dscr("QT", [NH, 2, 65, S])
        self.KT = self.dscr("KT", [NH, 2, 65, S])
        self.Vt = self.dscr("Vt", [NH, S, 128])
        self.U = self.dscr("U", [S, 1024])
        self.GAT = self.dscr("GAT", [D, S])
        self.GST = self.dscr("GST", [D, S])
        self.ATT = self.dscr("ATT", [1024, S])
        self.Gs = self.dscr("Gs", [S, 1024])
        self.Xs = self.dscr("Xs", [S, D], F32)
        self.CTAB = self.dscr("CTAB", [128, S], F32)
        self.STAB = self.dscr("STAB", [128, S], F32)

        if "DMT" in self.dbg:
            self.DMT = self.dscr("DMT", [S // 512, 128, 16, 512])
            self.DGL = self.dscr("DGL", [S // 512, 128, 8, 512])
            self.DX1 = self.dscr("DX1", [S, D], F32)
        self.arena = self.sb("arena", [128, ARENA_KB * 512], BF16)
        self.cst = self.sb("cstsb", [128, CW], F32)
        self.cb = self.sb("cstbf", [128, 544], BF16)
        self.S = Sched(lambda name: es.enter_context(nc.semaphore(name)))
        self.identf = self.cst[:, C_ID:C_ID + 128]
        self.identb = self.cb[:, C_ID:C_ID + 128]
        self.protb = self.cb[:, C_PROT:C_PROT + 128]
        self.trib = self.cb[:, C_TRI:C_TRI + 128]
        self.bonesb = self.cb[:, C_BONES:C_BONES + 2]

        self.run_phase(self.phase0, 0, 0)
        for l in range(NL):
            src = self.x if l == 0 else self.Xs
            dst = self.y if l == NL - 1 else self.Xs
            self.run_phase(lambda: self.phase1(l, src), 5, 3)
            self.run_phase(lambda: self.phase2(l), 5, 3)
            self.run_phase(lambda: self.phase3(l), 7, 1)
            self.run_phase(lambda: self.phase4(l, src, dst), 6, 2)
        return nc

    def run_phase(self, fn, nf32, nbf):
        self.phase_id = getattr(self, "phase_id", 0) + 1
        with ExitStack() as pes:
            self.ps = [pes.enter_context(self.nc.psum_tensor("ps%d_%d" % (self.phase_id, i), [128, 512], F32))
                       for i in range(nf32)]
            self.pb = [pes.enter_context(self.nc.psum_tensor("pb%d_%d" % (self.phase_id, i), [128, 512], BF16))
                       for i in range(nbf)]
            fn()
            self.S.barrier()
            self.S.replay(self.nc)

    def cast_weight_A(self, dst, src, K, N, kcpt, jper, key):
        ng = N // (128 * jper)
        v = src.rearrange("(kc d) (j c) -> j d kc c", d=128, c=128)
        for g in range(ng):
            for jj in range(jper):
                self.cast_pieces.append((dst[g][:, jj, :, :], v[g * jper + jj], kcpt, key))

    def phase0(self):
        S, NL = self.S_tok, self.NL
        self.LD(self.cst[:], self.cst_d, (), ("cst",))
        self.V(lambda e: e.tensor_copy(self.cb[:], self.cst[:, 0:544]), ("cst",), ("cb",))
        self.cast_pieces = []
        for l in range(NL):
            key = "Wb%d" % l
            self.cast_weight_A(self.Win_b[l], self.w_in[l], D, 8192, 16, 4, key)
            self.cast_weight_A(self.Wglu_b[l], self.w_glu[l], 1024, 1024, 8, 8, key)
            self.cast_weight_A(self.Wau_b[l], self.w_au[l], 1024, D, 8, 8, key)
            self.cast_weight_A(self.Wsu_b[l], self.w_su[l], 1024, D, 8, 8, key)
            vo = self.w_out[l].rearrange("(kc d) (n c) -> n d kc c", d=128, c=512)
            for n in range(4):
                for cc in range(4):
                    self.cast_pieces.append((self.Wout_b[l][n][:, :, cc * 128:(cc + 1) * 128],
                                             vo[n][:, :, cc * 128:(cc + 1) * 128], 16, key))
            self.cast_weight_A(self.W1_b[l], self.w1[l], D, 8192, 16, 4, key)
            v2 = self.w2[l].rearrange("(kg kc f) (n c) -> n kg f kc c", kc=16, f=128, c=512)
            for n in range(4):
                for kg in range(4):
                    for cc in range(4):
                        self.cast_pieces.append((self.W2_b[l][n][kg][:, :, cc * 128:(cc + 1) * 128],
                                                 v2[n][kg][:, :, cc * 128:(cc + 1) * 128], 16, key))
        self.carve_reset()
        NSL = 8
        stg = [self.carve([16, 128], BF16) for _ in range(NSL)]
        pcs = self.cast_pieces

        def cload(i):
            if i < len(pcs):
                d, sv, kcpt, key = pcs[i]
                self.S.dma("gpsimd", stg[i % NSL][:, 0:kcpt, :], sv, (), ("stg%d" % (i % NSL),))
        for i in range(min(NSL - 2, len(pcs))):
            cload(i)
        for i in range(len(pcs)):
            cload(i + NSL - 2)
            d, sv, kcpt, key = pcs[i]
            self.LD(d, stg[i % NSL][:, 0:kcpt, :], ("stg%d" % (i % NSL),), (), sem="castst")
        TC = min(S, 2048)
        posi = self.carve([TC], I32)
        posf = self.carve([TC], F32)
        ang = self.carve([TC], F32)
        tmp = self.carve([TC], F32)
        res = self.carve([TC], F32)
        inv = self.cst[:, C_INV:C_INV + 1]
        for t in range(S // TC):
            sl = slice(t * TC, (t + 1) * TC)
            self.LD(posi, self.pos[:, sl].partition_broadcast(128), (), ("posi",))
            self.V(lambda e: e.tensor_copy(posf, posi), ("posi",), ("posf",))
            for tab, shift in ((self.CTAB, math.pi / 2), (self.STAB, 0.0)):
                self.V(lambda e, shift=shift: e.tensor_scalar(ang, posf, inv, shift, ALU.mult, ALU.add),
                       ("posf", "cst"), ("rr",))
                self.sin_rr(res, ang, tmp, ("rr",))
                self.ST(tab[:, sl], res, ("rr",), (), sem="tabst")

    def load_xT(self, src, t0, xin, xb, xT, key="x"):
        self.LD(xin, src[t0:t0 + 512, :].rearrange("(a p) d -> p a d", p=128), (), (key + "in",))
        for a in range(4):
            if a % 2 == 0:
                self.A(lambda e, a=a: e.copy(xb[:, a, :], xin[:, a, :]), (key + "in",), (key + "b%d" % a,))
            else:
                self.V(lambda e, a=a: e.tensor_copy(xb[:, a, :], xin[:, a, :]), (key + "in",), (key + "b%d" % a,))
        self.transpose_block(xb, xT, 16, key)

    def transpose_block(self, xb, xT, nkc, key, extra_r=(), outkey=None):
        for kc in range(nkc):
            half = kc % 2
            dstp = self.pb[half][:]
            self.PE([lambda e, a=a, kc=kc, dstp=dstp: e.transpose(dstp[:, a * 128:(a + 1) * 128],
                                                                  xb[:, a, kc * 128:(kc + 1) * 128], self.identb)
                     for a in range(4)],
                    [key + "b%d" % a for a in range(4)] + ["cb"] + list(extra_r), ("ptr%d" % half,))
            ok = outkey if outkey is not None else key + "T"
            if kc % 2 == 0:
                self.V(lambda e, kc=kc, dstp=dstp: e.tensor_copy(xT[:, kc, :], dstp), ("ptr%d" % half,), (ok,))
            else:
                self.A(lambda e, kc=kc, dstp=dstp: e.copy(xT[:, kc, :], dstp), ("ptr%d" % half,), (ok,))

    def phase1(self, l, src):
        S = self.S_tok
        NB = S // 512
        self.carve_reset()
        xin = self.carve([4, D], F32)
        xb = self.carve([4, D], BF16)
        xT = self.carve([16, 512], BF16)
        wr = [self.carve([4, 16, 128], BF16) for _ in range(3)]
        ct = self.carve([512], F32)
        st = self.carve([512], F32)
        qb = [self.carve([512], BF16) for _ in range(2)]
        t1 = [self.carve([512], F32) for _ in range(2)]
        t2 = [self.carve([512], F32) for _ in range(2)]
        qr = [self.carve([512], BF16) for _ in range(2)]
        sq = [self.carve([512], BF16) for _ in range(2)]
        ev = [self.carve([512], BF16) for _ in range(4)]
        tk = [self.carve([4, 128], BF16) for _ in range(2)]
        nrm = [self.carve([512], BF16) for _ in range(2)]
        nk = self.carve([512], F32)
        kmx = self.carve([16], F32)
        krow = self.carve([S], BF16)
        Wg = self.Win_b[l]
        wkey = "Wb%d" % l
        self.V(lambda e: e.memset(kmx[0:2, :], 0.0), (), ("kmx",))
        cnt = {"q": 0, "ev": 0, "tk": 0}
        wloads = [(t, g) for t in range(NB) for g in range(16)]

        def wload(i):
            if i < len(wloads):
                t, g = wloads[i]
                self.LD(wr[i % 3], Wg[g], (wkey,), ("w%d" % (i % 3),))
        if getattr(self, "lvl", 9) >= 1:
            wload(0)
            wload(1)
        for t in range(NB):
            t0 = t * 512
            self.load_xT(src, t0, xin, xb, xT)
            self.LD(ct, self.CTAB[:, t0:t0 + 512], (), ("ct",))
            self.LD(st, self.STAB[:, t0:t0 + 512], (), ("st",))
            for g in range(16):
                if getattr(self, "lvl", 9) < 1:
                    break
                wi = t * 16 + g
                wload(wi + 2)
                wt = wr[wi % 3]
                for jj in range(4):
                    j = g * 4 + jj
                    bank = j % 3
                    pm = self.ps[bank][:]
                    self.PE([lambda e, kc=kc, wt=wt, jj=jj, pm=pm: e.matmul(pm, wt[:, jj, kc, :], xT[:, kc, :],
                                                                             start=(kc == 0), stop=(kc == 15))
                             for kc in range(16)], ("w%d" % (wi % 3), "xT"), ("pm%d" % bank,))
                    pk = "pm%d" % bank
                    lvl = getattr(self, "lvl", 9)
                    if (j < 16 and lvl < 5) or (16 <= j < 32 and lvl < 4) or (j >= 32 and lvl < 3):
                        continue
                    if j < 16:
                        i = cnt["q"] % 2
                        cnt["q"] += 1
                        h = j % 8
                        dstT = self.QT if j < 8 else self.KT
                        self.A(lambda e, i=i, pm=pm: e.copy(qb[i], pm), (pk,), ("qb%d" % i,))
                        prot = self.ps[3][:]
                        self.PE([lambda e, i=i, prot=prot: e.matmul(prot, self.protb, qb[i], start=True, stop=True)],
                                ("qb%d" % i, "cb"), ("prot",))
                        self.V(lambda e, i=i, pm=pm: e.tensor_tensor(t1[i], pm, ct, ALU.mult), (pk, "ct", "qb%d" % i), ("t1%d" % i,))
                        self.V(lambda e, i=i, prot=prot: e.tensor_tensor(t2[i], prot, st, ALU.mult), ("prot", "st"), ("t2%d" % i,))
                        self.V(lambda e, i=i: e.tensor_tensor(qr[i], t1[i], t2[i], ALU.add),
                               ("t1%d" % i, "t2%d" % i), ("qr%d" % i,))
                        for c in range(2):
                            self.ST(dstT[h, c, 0:64, t0:t0 + 512], qr[i][c * 64:(c + 1) * 64, :], ("qr%d" % i,), ())
                        if lvl < 6:
                            continue
                        self.V(lambda e, i=i: e.tensor_tensor(sq[i], qr[i], qr[i], ALU.mult), ("qr%d" % i,), ("sq%d" % i,))
                        pn = self.ps[4][0:2, :]
                        self.PE([lambda e, i=i, pn=pn: e.matmul(pn, self.bonesb, sq[i], start=True, stop=True)],
                                ("sq%d" % i, "cb"), ("pn",))
                        if lvl < 7:
                            continue
                        if j < 8:
                            self.A(lambda e, i=i, pn=pn: e.activation(nk[0:2, :], pn, AF.Sqrt), ("pn",), ("nk",))
                            self.V(lambda e, i=i: e.tensor_single_scalar(nrm[i][0:2, :], nk[0:2, :], -1.0, ALU.mult),
                                   ("nk",), ("nrm%d" % i,))
                            self.ST(self.QT[h, :, 64, t0:t0 + 512], nrm[i][0:2, :], ("nrm%d" % i,), ())
                        elif lvl >= 8:
                            self.V(lambda e, pn=pn: e.reduce_max(nk[0:2, 0:1], pn, mybir.AxisListType.X), ("pn",), ("nk",))
                            self.V(lambda e, h=h: e.tensor_tensor(kmx[0:2, h:h + 1], kmx[0:2, h:h + 1], nk[0:2, 0:1], ALU.max),
                                   ("nk", "kmx"), ("kmx",))
                    elif j < 32:
                        i = cnt["ev"] % 4
                        cnt["ev"] += 1
                        self.A(lambda e, i=i, pm=pm: e.copy(ev[i], pm), (pk,), ("ev%d" % i,))
                        ptk = self.pb[2][:]
                        self.PE([lambda e, a=a, i=i, ptk=ptk: e.transpose(ptk[:, a * 128:(a + 1) * 128],
                                                                          ev[i][:, a * 128:(a + 1) * 128], self.identb)
                                 for a in range(4)], ("ev%d" % i, "cb"), ("ptk",))
                        i2 = cnt["tk"] % 2
                        cnt["tk"] += 1
                        self.V(lambda e, i2=i2, ptk=ptk: e.tensor_copy(tk[i2], ptk.rearrange("p (a c) -> p a c", a=4)),
                               ("ptk",), ("tk%d" % i2,))
                        if j < 24:
                            dv = self.Vt[j - 16, t0:t0 + 512, :].rearrange("(a p) e -> p a e", p=128)
                        else:
                            dv = self.U[t0:t0 + 512, (j - 24) * 128:(j - 23) * 128].rearrange("(a p) c -> p a c", p=128)
                        self.ST(dv, tk[i2], ("tk%d" % i2,), ())
                    else:
                        i = cnt["ev"] % 4
                        cnt["ev"] += 1
                        self.A(lambda e, i=i, pm=pm: e.activation(ev[i], pm, AF.Sigmoid), (pk,), ("ev%d" % i,))
                        if j < 48:
                            dg = self.GAT[(j - 32) * 128:(j - 31) * 128, t0:t0 + 512]
                        else:
                            dg = self.GST[(j - 48) * 128:(j - 47) * 128, t0:t0 + 512]
                        self.ST(dg, ev[i], ("ev%d" % i,), ())
        if getattr(self, "lvl", 9) < 9:
            return
        self.A(lambda e: e.activation(kmx[0:2, 8:16], kmx[0:2, 0:8], AF.Sqrt), ("kmx",), ("kmx2",))
        for h in range(NH):
            self.V(lambda e, h=h: e.memset(krow[0:2, :], 1.0), (), ("krow",))
            self.V(lambda e, h=h: e.tensor_scalar(krow[0:2, :], krow[0:2, :], kmx[0:2, 8 + h:9 + h], None, ALU.mult),
                   ("krow", "kmx2"), ("krow",))
            self.ST(self.KT[h, :, 64, :], krow[0:2, :], ("krow",), ())

    def phase2(self, l):
        S = self.S_tok
        NCH = S // 8
        CB = min(512, NCH)
        NHALF = NCH // CB
        NSEG = CB // 128
        NBLK = S // 1024
        X = mybir.AxisListType.X
        self.carve_reset()
        t64 = lambda: self.carve([64], F32)
        Xa = self.carve([128], F32)
        Xi = self.carve([128], F32)
        XD = self.carve([8, 16], F32)
        Are, Aim, dtb, phi, lm, mag, cs, sn, Abr, Abi = [t64() for _ in range(10)]
        den, nr, zr, zi, zis, ta, tb, thr, ths, r8, Ivr, Ivi, Dv, ang, tmp = [t64() for _ in range(15)]
        PWR = self.carve([16, 64], F32)
        PWI = self.carve([16, 64], F32)
        PWS = self.carve([16, 64], F32)
        X1 = self.carve([64, 16], F32)
        X2 = self.carve([64, 16], F32)
        BB1 = self.carve([64, 16], F32)
        BB2 = self.carve([64, 16], F32)
        T1 = self.carve([64, 16], F32)
        T2 = self.carve([64, 16], F32)
        PBt = self.carve([64, 15, 16], BF16)
        PC = self.carve([64, 9, 16], BF16)
        Ug = self.carve([NBLK, 128], BF16)
        Utok = self.carve([NBLK, 8, 128], BF16)
        Ytok = self.carve([NBLK, 8, 128], BF16)
        UT = self.carve([NCH], BF16)
        W = CB + 1
        TCt, TSt, an2, tm2 = [self.carve([W], F32) for _ in range(4)]
        a1, a2, So, Sx, rt, gbo, gbx, ysb, y2, sgt = [self.carve([W], F32) for _ in range(10)]
        gl = self.carve([CB], BF16)
        Hb = self.carve([CB], BF16)
        Tsb = self.carve([128], BF16)
        SBs = self.carve([128], BF16)
        SBs2 = self.carve([128], BF16)
        tmpT = self.carve([128], F32)
        sga = self.cst[:, C_SGA:C_SGA + 1]
        sgb = self.cst[:, C_SGB:C_SGB + 1]
        P = ("prm",)

        def vv(fn):
            self.V(fn, P, P)

        def cmul(outr, outi, ar, ai, br, bi):
            vv(lambda e: e.tensor_tensor(ta, ar, br, ALU.mult))
            vv(lambda e: e.tensor_tensor(tb, ai, bi, ALU.mult))
            vv(lambda e: e.tensor_tensor(outr, ta, tb, ALU.subtract))
            vv(lambda e: e.tensor_tensor(ta, ar, bi, ALU.mult))
            vv(lambda e: e.tensor_tensor(tb, ai, br, ALU.mult))
            vv(lambda e: e.tensor_tensor(outi, ta, tb, ALU.add))
        for (xt, srcp) in ((Xa, self.a_re[l]), (Xi, self.a_im[l])):
            self.LD(xt[0:64, 0:64], srcp, (), P)
            self.LD(xt[0:64, 64:128], srcp, (), P)
        self.LD(dtb, self.log_dt[l:l + 1, :].partition_broadcast(128), (), P)
        for m in range(8):
            self.LD(XD[0:64, m, :], self.ssm_d[l], (), P)
        pm0 = self.ps[0][:]
        for (xt, dstt) in ((Xa, Are), (Xi, Aim)):
            self.PE([lambda e, xt=xt: e.transpose(pm0[:, 0:64], xt[0:64, :], self.identf[0:64, 0:64])], P + ("cst",), ("pm0",))
            self.V(lambda e, dstt=dstt: e.tensor_copy(dstt, pm0[:, 0:64]), ("pm0",) + P, P)
        self.PE([lambda e: e.transpose(pm0[:, 0:64], XD[0:64, :, :].rearrange("p a b -> p (a b)"), self.identf[0:64, 0:64])],
                P + ("cst",), ("pm0",))
        self.V(lambda e: e.tensor_copy(Dv, pm0[:, 0:64]), ("pm0",) + P, P)
        self.A(lambda e: e.activation(dtb, dtb, AF.Exp), P, P)
        vv(lambda e: e.tensor_tensor(phi, dtb, Aim, ALU.mult))
        vv(lambda e: e.tensor_tensor(lm, dtb, Are, ALU.mult))
        self.A(lambda e: e.activation(mag, lm, AF.Exp), P, P)
        self.A(lambda e: e.activation(r8, lm, AF.Exp, scale=8.0), P, P)
        vv(lambda e: e.tensor_single_scalar(ang, phi, math.pi / 2, ALU.add))
        self.sin_rr(cs, ang, tmp, P)
        vv(lambda e: e.tensor_copy(ang, phi))
        self.sin_rr(sn, ang, tmp, P)
        vv(lambda e: e.tensor_tensor(Abr, mag, cs, ALU.mult))
        vv(lambda e: e.tensor_tensor(Abi, mag, sn, ALU.mult))
        vv(lambda e: e.tensor_single_scalar(thr, phi, 8.0, ALU.mult))
        vv(lambda e: e.tensor_scalar(tmp, thr, 1.0 / TWO_PI, MAGIC, ALU.mult, ALU.add))
        vv(lambda e: e.tensor_single_scalar(tmp, tmp, MAGIC, ALU.subtract))
        vv(lambda e: e.scalar_tensor_tensor(thr, tmp, -CW1, thr, ALU.mult, ALU.add))
        vv(lambda e: e.scalar_tensor_tensor(thr, tmp, -CW2, thr, ALU.mult, ALU.add))
        vv(lambda e: e.tensor_scalar(ths, thr, sgb, None, ALU.mult))
        vv(lambda e: e.tensor_tensor(ta, Are, Are, ALU.mult))
        vv(lambda e: e.tensor_tensor(tb, Aim, Aim, ALU.mult))
        vv(lambda e: e.tensor_tensor(den, ta, tb, ALU.add))
        vv(lambda e: e.reciprocal(den, den))
        vv(lambda e: e.tensor_single_scalar(nr, Abr, -1.0, ALU.add))
        vv(lambda e: e.tensor_tensor(ta, nr, Are, ALU.mult))
        vv(lambda e: e.tensor_tensor(tb, Abi, Aim, ALU.mult))
        vv(lambda e: e.tensor_tensor(ta, ta, tb, ALU.add))
        vv(lambda e: e.tensor_tensor(zr, ta, den, ALU.mult))
        vv(lambda e: e.tensor_tensor(ta, Abi, Are, ALU.mult))
        vv(lambda e: e.tensor_tensor(tb, nr, Aim, ALU.mult))
        vv(lambda e: e.tensor_tensor(ta, ta, tb, ALU.subtract))
        vv(lambda e: e.tensor_tensor(zi, ta, den, ALU.mult))
        vv(lambda e: e.tensor_scalar(zis, zi, sgb, None, ALU.mult))
        vv(lambda e: e.tensor_tensor(ta, Abr, Abr, ALU.mult))
        vv(lambda e: e.tensor_tensor(tb, Abi, Abi, ALU.mult))
        vv(lambda e: e.tensor_tensor(ta, ta, tb, ALU.add))
        vv(lambda e: e.reciprocal(ta, ta))
        vv(lambda e: e.tensor_tensor(Ivr, Abr, ta, ALU.mult))
        vv(lambda e: e.scalar_tensor_tensor(Ivi, Abi, -1.0, ta, ALU.mult, ALU.mult))
        vv(lambda e: e.memset(PWR[:, 7, :], 1.0))
        vv(lambda e: e.memset(PWI[:, 7, :], 0.0))
        for k in range(1, 9):
            cmul(PWR[:, 7 + k, :], PWI[:, 7 + k, :], PWR[:, 6 + k, :], PWI[:, 6 + k, :], Abr, Abi)
        for k in range(1, 8):
            cmul(PWR[:, 7 - k, :], PWI[:, 7 - k, :], PWR[:, 8 - k, :], PWI[:, 8 - k, :], Ivr, Ivi)
        vv(lambda e: e.tensor_scalar(PWS, PWI, sgb, None, ALU.mult))
        self.LD(X1[0:64], self.b_re[l].rearrange("g p i -> p g i"), (), P)
        self.LD(X1[64:128], self.b_im[l].rearrange("g p i -> p g i"), (), P)
        self.LD(X2[0:64], self.b_im[l].rearrange("g p i -> p g i"), (), P)
        self.LD(X2[64:128], self.b_re[l].rearrange("g p i -> p g i"), (), P)
        bc = lambda t: t.unsqueeze(2).to_broadcast([128, 64, 16])
        vv(lambda e: e.tensor_tensor(T1, X1, bc(zr), ALU.mult))
        vv(lambda e: e.tensor_tensor(T2, X2, bc(zis), ALU.mult))
        vv(lambda e: e.tensor_tensor(BB1, T1, T2, ALU.add))
        vv(lambda e: e.tensor_tensor(T1, X2, bc(zr), ALU.mult))
        vv(lambda e: e.tensor_tensor(T2, X1, bc(zis), ALU.mult))
        vv(lambda e: e.tensor_tensor(BB2, T1, T2, ALU.subtract))
        for kk in range(15):
            ki = 14 - kk
            vv(lambda e, ki=ki: e.tensor_tensor(T1, BB1, bc(PWR[:, ki, :]), ALU.mult))
            vv(lambda e, ki=ki: e.tensor_tensor(T2, BB2, bc(PWS[:, ki, :]), ALU.mult))
            vv(lambda e, kk=kk: e.tensor_tensor(PBt[:, :, kk, :], T1, T2, ALU.add))
        CC1 = X1.rearrange("p g i -> p (g i)").rearrange("p (j c) -> p j c", j=8)
        CC2 = X2.rearrange("p g i -> p (g i)").rearrange("p (j c) -> p j c", j=8)
        Y1 = BB1.rearrange("p g i -> p (g i)")
        Y2 = BB2.rearrange("p g i -> p (g i)")
        cre = self.c_re[l].rearrange("g o p -> (g o) p").rearrange("(j r) p -> r j p", r=128)
        cim = self.c_im[l].rearrange("g o p -> (g o) p").rearrange("(j r) p -> r j p", r=128)
        self.LD(CC1[:, :, 0:64], cre, (), P)
        self.LD(CC1[:, :, 64:128], cim, (), P)
        self.LD(CC2[:, :, 0:64], cim, (), P)
        self.LD(CC2[:, :, 64:128], cre, (), P)
        for jt in range(8):
            self.PE([lambda e, jt=jt: e.transpose(pm0[:, 0:128], CC1[:, jt, :], self.identf)], P + ("cst",), ("pm0",))
            self.V(lambda e, jt=jt: e.tensor_scalar(Y1[:, jt * 128:(jt + 1) * 128], pm0[:, 0:128], sga, None, ALU.mult), ("pm0",) + P, P)
            self.PE([lambda e, jt=jt: e.transpose(pm0[:, 0:128], CC2[:, jt, :], self.identf)], P + ("cst",), ("pm0",))
            self.V(lambda e, jt=jt: e.tensor_copy(Y2[:, jt * 128:(jt + 1) * 128], pm0[:, 0:128]), ("pm0",) + P, P)
        for k in range(9):
            vv(lambda e, k=k: e.tensor_tensor(T1, BB1, bc(PWR[:, 7 + k, :]), ALU.mult))
            vv(lambda e, k=k: e.tensor_tensor(T2, BB2, bc(PWI[:, 7 + k, :]), ALU.mult))
            vv(lambda e, k=k: e.tensor_tensor(PC[:, :, k, :], T1, T2, ALU.subtract))
        ssmmask = self.cst[:, C_SSM:C_SSM + 128]
        for bt_ in range(8):
            uv = self.U[:, bt_ * 128:(bt_ + 1) * 128].rearrange("(b c t) ch -> b c t ch", c=128, t=8)
            for blk in range(NBLK):
                self.LD(Utok[:, blk, :, :], uv[blk], (), ("Utok",))
            for gb in range(8):
                g = bt_ * 8 + gb
                self.G(lambda e, gb=gb: e.tensor_copy(Ug.rearrange("p b (t i) -> p b t i", t=8), Utok[:, :, :, gb * 16:(gb + 1) * 16]),
                       ("Utok",), ("Ug",))
                for blk in range(NBLK):
                    q4 = blk % 4
                    self.PE([lambda e, blk=blk, q4=q4, gb=gb: e.transpose(self.pb[0][:, q4 * 128:(q4 + 1) * 128],
                                                                         Ug[:, blk, :], self.identb)],
                            ("Ug", "cb"), ("pbU",))
                    if q4 == 3 or blk == NBLK - 1:
                        b0 = blk - q4
                        self.A(lambda e, b0=b0, q4=q4: e.copy(UT[:, b0 * 128:(b0 + q4 + 1) * 128], self.pb[0][:, 0:(q4 + 1) * 128]),
                               ("pbU",), ("UT",))
                self.PE([lambda e, g=g: e.matmul(pm0[:, 0:128], PBt[:, g, 7:15, :].rearrange("p a b -> p (a b)"), PC[:, g, 0:8, :].rearrange("p a b -> p (a b)"), start=True, stop=True)],
                        P, ("pm0",))
                self.V(lambda e: e.tensor_tensor(tmpT, pm0[:, 0:128], ssmmask, ALU.mult), ("pm0", "cst"), ("tmpT",))
                self.V(lambda e, g=g: e.scalar_tensor_tensor(Tsb, self.identf, Dv[:, g:g + 1], tmpT, ALU.mult, ALU.add),
                       ("tmpT", "cst") + P, ("Tsb",))
                self.PE([lambda e, g=g: e.transpose(self.pb[1][:, 0:128], PBt[:, g, 0:8, :].rearrange("p a b -> p (a b)"), self.identb)], P + ("cb",), ("pbS",))
                self.V(lambda e: e.tensor_copy(SBs, self.pb[1][:, 0:128]), ("pbS",), ("SBs",))
                self.V(lambda e: e.tensor_copy(SBs2[:, 0:64], self.pb[1][:, 64:128]), ("pbS",), ("SBs2",))
                self.V(lambda e: e.tensor_copy(SBs2[:, 64:128], self.pb[1][:, 0:64]), ("pbS",), ("SBs2",))
                for hh in range(NHALF):
                    cols = slice(hh * CB, (hh + 1) * CB)
                    io = self.cst[:, C_IOTA + hh * CB:C_IOTA + hh * CB + W]
                    pS1 = self.ps[1][:, 0:CB]
                    pS2 = self.ps[2][:, 0:CB]
                    pY = self.ps[3][:, 0:CB]
                    self.PE([lambda e, cols=cols: e.matmul(pS1, SBs, UT[:, cols], start=True, stop=True)], ("SBs", "UT"), ("pS1",))
                    self.PE([lambda e, cols=cols: e.matmul(pS2, SBs2, UT[:, cols], start=True, stop=True)], ("SBs2", "UT"), ("pS2",))
                    self.V(lambda e, g=g, io=io: e.tensor_scalar(an2, io, thr[:, g:g + 1], math.pi / 2, ALU.mult, ALU.add),
                           ("cst", "tabs") + P, ("tabc",))
                    self.sin_rr(TCt, an2, tm2, ("tabc",))
                    self.V(lambda e, g=g, io=io: e.tensor_scalar(an2, io, ths[:, g:g + 1], None, ALU.mult), ("cst", "tabc") + P, ("tabs",))
                    self.sin_rr(TSt, an2, tm2, ("tabs",))
                    tc1, ts1 = TCt[:, 1:W], TSt[:, 1:W]
                    self.V(lambda e: e.tensor_tensor(a1[:, 0:CB], tc1, pS1, ALU.mult), ("tabc", "pS1"), ("a1",))
                    self.V(lambda e: e.tensor_tensor(a2[:, 0:CB], ts1, pS2, ALU.mult), ("tabs", "pS2"), ("a2",))
                    self.V(lambda e: e.tensor_tensor(So[:, 0:CB], a1[:, 0:CB], a2[:, 0:CB], ALU.subtract), ("a1", "a2"), ("So",))
                    self.V(lambda e: e.tensor_tensor(a1[:, 0:CB], tc1, pS2, ALU.mult), ("tabc", "pS2", "So"), ("a1",))
                    self.V(lambda e: e.tensor_tensor(a2[:, 0:CB], ts1, pS1, ALU.mult), ("tabs", "pS1", "So"), ("a2",))
                    self.V(lambda e: e.tensor_tensor(Sx[:, 0:CB], a1[:, 0:CB], a2[:, 0:CB], ALU.add), ("a1", "a2"), ("Sx",))
                    self.V(lambda e, g=g, io=io: e.tensor_scalar(rt[:, 0:CB], io[:, 0:CB], 0.0, r8[:, g:g + 1], ALU.mult, ALU.add),
                           ("cst",) + P, ("rt",))
                    if hh == 0:
                        self.V(lambda e: e.memset(gbo[:, 0:1], 0.0), ("gb",), ("gb",))
                        self.V(lambda e: e.memset(gbx[:, 0:1], 0.0), ("gb",), ("gb",))
                    self.V(lambda e: e.tensor_tensor_scan(gbo[:, 1:W], rt[:, 0:CB], So[:, 0:CB], gbo[:, 0:1], ALU.mult, ALU.add),
                           ("rt", "So", "gb"), ("gb",))
                    self.V(lambda e: e.tensor_tensor_scan(gbx[:, 1:W], rt[:, 0:CB], Sx[:, 0:CB], gbx[:, 0:1], ALU.mult, ALU.add),
                           ("rt", "Sx", "gb"), ("gb",))
                    self.V(lambda e: e.tensor_tensor(a1[:, 0:CB], TCt[:, 0:CB], gbo[:, 0:CB], ALU.mult), ("tabc", "gb", "Sx"), ("a1",))
                    self.V(lambda e: e.tensor_tensor(a2[:, 0:CB], TSt[:, 0:CB], gbx[:, 0:CB], ALU.mult), ("tabs", "gb", "Sx"), ("a2",))
                    self.V(lambda e: e.tensor_tensor(Hb, a1[:, 0:CB], a2[:, 0:CB], ALU.add), ("a1", "a2"), ("Hb",))
                    if hh + 1 < NHALF:
                        self.V(lambda e: e.tensor_copy(gbo[:, 0:1], gbo[:, CB:W]), ("gb", "a1"), ("gb",))
                        self.V(lambda e: e.tensor_copy(gbx[:, 0:1], gbx[:, CB:W]), ("gb", "a2"), ("gb",))
                    self.PE([lambda e, cols=cols: e.matmul(pY, Tsb, UT[:, cols], start=True, stop=False),
                             lambda e, g=g: e.matmul(pY, PC[:, g, 1:9, :].rearrange("p a b -> p (a b)"), Hb, start=False, stop=True)],
                            ("Tsb", "UT", "Hb") + P, ("pY",))
                    self.A(lambda e: e.copy(ysb[:, 0:CB], pY), ("pY",), ("ysb",))
                    self.V(lambda e: e.tensor_tensor(y2[:, 0:CB], ysb[:, 0:CB], ysb[:, 0:CB], ALU.mult), ("ysb",), ("y2",))
                    self.V(lambda e: e.tensor_scalar(y2[:, 0:CB], y2[:, 0:CB], 0.044715, 1.0, ALU.mult, ALU.add), ("y2",), ("y2",))
                    self.V(lambda e: e.tensor_tensor(y2[:, 0:CB], y2[:, 0:CB], ysb[:, 0:CB], ALU.mult), ("y2", "ysb"), ("y2",))
                    self.A(lambda e: e.activation(sgt[:, 0:CB], y2[:, 0:CB], AF.Sigmoid, scale=GELU_C), ("y2",), ("sgt",))
                    self.V(lambda e: e.tensor_tensor(gl, ysb[:, 0:CB], sgt[:, 0:CB], ALU.mult), ("ysb", "sgt"), ("gl",))
                    for seg in range(NSEG):
                        self.PE([lambda e, seg=seg: e.transpose(self.pb[2][:, seg * 128:(seg + 1) * 128],
                                                                gl[:, seg * 128:(seg + 1) * 128], self.identb)],
                                ("gl", "cb"), ("pbY",))
                    b0 = hh * NSEG
                    self.A(lambda e, b0=b0, gb=gb: e.copy(Ytok[:, b0:b0 + NSEG, :, gb * 16:(gb + 1) * 16],
                                                          self.pb[2][:, 0:NSEG * 128].rearrange("p (s t o) -> p s t o", s=NSEG, t=8)),
                           ("pbY",), ("Ytok",))
            gv = self.Gs[:, bt_ * 128:(bt_ + 1) * 128].rearrange("(b c t) ch -> b c t ch", c=128, t=8)
            for blk in range(NBLK):
                self.ST(gv[blk], Ytok[:, blk, :, :], ("Ytok",), ())

    def phase3(self, l):
        S = self.S_tok
        NKB = S // 128
        NSB = S // 256
        self.carve_reset()
        lp = self.carve([2], F32)
        self.LD(lp, self.lparams[l:l + 1, :].partition_broadcast(128), (), ("lp",))
        kt = self.carve([2, S], BF16)
        vt = self.carve([NKB, 144], BF16)
        qt = [self.carve([2, 256], BF16) for _ in range(2)]
        pt = [self.carve([512], BF16) for _ in range(3)]
        lq = self.carve([256], F32)
        ltmp = self.carve([64], F32)
        lsm = self.carve([8], F32)
        gsub = self.carve([128], F32)
        rc = [self.carve([4], F32) for _ in range(2)]
        dd = [self.carve([128], F32) for _ in range(2)]
        sqq = [self.carve([128], F32) for _ in range(2)]
        ssm = [self.carve([4], F32) for _ in range(2)]
        ob = [self.carve([128], BF16) for _ in range(2)]
        oT = [self.carve([256], BF16) for _ in range(2)]
        self.LD(lq, self.lambda_qk[l:l + 1].rearrange("o a b -> o (a b)").partition_broadcast(128), (), ("lq",))
        self.LD(gsub, self.subln_g[l:l + 1, :].partition_broadcast(128), (), ("gsub",))
        for i in range(2):
            self.V(lambda e, i=i: e.tensor_tensor(ltmp, lq[:, i * 128:i * 128 + 64], lq[:, i * 128 + 64:i * 128 + 128], ALU.mult),
                   ("lq",), ("ltmp",))
            self.V(lambda e, i=i: e.reduce_sum(lsm[:, i:i + 1], ltmp, mybir.AxisListType.X), ("ltmp",), ("lsm",))
        self.A(lambda e: e.activation(lsm[:, 2:4], lsm[:, 0:2], AF.Exp), ("lsm",), ("lsm2",))
        self.V(lambda e: e.tensor_tensor(lsm[:, 4:5], lsm[:, 2:3], lsm[:, 3:4], ALU.subtract), ("lsm2",), ("lsm3",))
        self.V(lambda e: e.tensor_scalar(lsm[:, 5:6], lsm[:, 4:5], lp[:, 0:1], -1.0, ALU.add, ALU.mult), ("lsm3", "lp"), ("lamneg",))
        lamneg = lsm[:, 5:6]
        self.V(lambda e: e.tensor_scalar(gsub, gsub, lp[:, 1:2], None, ALU.mult), ("gsub", "lp"), ("gsub",))
        self.V(lambda e: e.memset(vt[:, :, 128:129], 1.0), (), ("vt1",))
        cnt = {"sc": 0, "pt": 0, "ep": 0}
        for h in range(NH):
            self.LD(kt[0:65, :, :], self.KT[h].rearrange("c r s -> r c s"), (), ("kt",))
            self.LD(vt[:, :, 0:128], self.Vt[h].rearrange("(j p) e -> p j e", p=128), (), ("vt",))
            self.LD(qt[0][0:65, :, :], self.QT[h][:, :, 0:256].rearrange("c r q -> r c q"), (), ("qt0",))
            for s in range(NSB):
                q0 = s * 256
                qi = s % 2
                if s + 1 < NSB:
                    self.LD(qt[(s + 1) % 2][0:65, :, :],
                            self.QT[h][:, :, q0 + 256:q0 + 512].rearrange("c r q -> r c q"), (), ("qt%d" % ((s + 1) % 2),))
                nj = 2 * s + 2

                def score(j, qi=qi):
                    b = cnt["sc"] % 2
                    cnt["sc"] += 1
                    psc = self.ps[b][:]
                    self.PE([lambda e, c=c, j=j, psc=psc, qi=qi: e.matmul(psc[:, c * 256:(c + 1) * 256],
                                                                         kt[0:65, c, j * 128:(j + 1) * 128],
                                                                         qt[qi][0:65, c, :], start=True, stop=True)
                             for c in range(2)], ("kt", "qt%d" % qi), ("sc%d" % b,))
                    return b
                bnext = score(0)
                for j in range(nj):
                    b = bnext
                    if j + 1 < nj:
                        bnext = score(j + 1)
                    pi = cnt["pt"] % 3
                    cnt["pt"] += 1
                    psc = self.ps[b][:]
                    self.A(lambda e, pi=pi, psc=psc: e.activation(pt[pi], psc, AF.Exp, scale=0.125),
                           ("sc%d" % b,), ("pt%d" % pi,))
                    for a in range(2):
                        jq = 2 * s + a
                        if j > jq:
                            continue
                        if j == jq:
                            for c in range(2):
                                sl = slice(c * 256 + a * 128, c * 256 + a * 128 + 128)
                                self.V(lambda e, pi=pi, sl=sl: e.tensor_tensor(pt[pi][:, sl], pt[pi][:, sl], self.trib, ALU.mult),
                                       ("pt%d" % pi, "cb"), ("pt%d" % pi,))
                        for c in range(2):
                            sl = slice(c * 256 + a * 128, c * 256 + a * 128 + 128)
                            oacc = self.ps[2 + 2 * a + c][:, 0:129]
                            self.PE([lambda e, pi=pi, sl=sl, oacc=oacc, j=j, jq=jq: e.matmul(
                                oacc, pt[pi][:, sl], vt[:, j, 0:129], start=(j == 0), stop=(j == jq))],
                                ("pt%d" % pi, "vt", "vt1"), ("O%d%d" % (a, c),))
                ei = cnt["ep"] % 2
                cnt["ep"] += 1
                ptr = self.pb[0][:]
                for a in range(2):
                    O0 = self.ps[2 + 2 * a][:, 0:129]
                    O1 = self.ps[3 + 2 * a][:, 0:129]
                    k0, k1 = "O%d0" % a, "O%d1" % a
                    ek = "ep%d" % a
                    self.V(lambda e, a=a, O0=O0: e.reciprocal(rc[a][:, 0:1], O0[:, 128:129]), (k0,), (ek,))
                    self.V(lambda e, a=a, O1=O1: e.reciprocal(rc[a][:, 1:2], O1[:, 128:129]), (k1, ek), (ek,))
                    self.V(lambda e, a=a: e.tensor_tensor(rc[a][:, 2:3], rc[a][:, 1:2], lamneg, ALU.mult), (ek, "lamneg"), (ek,))
                    self.V(lambda e, a=a, O0=O0: e.tensor_scalar(dd[a], O0[:, 0:128], rc[a][:, 0:1], None, ALU.mult), (k0, ek), (ek,))
                    self.V(lambda e, a=a, O1=O1: e.scalar_tensor_tensor(dd[a], O1[:, 0:128], rc[a][:, 2:3], dd[a], ALU.mult, ALU.add),
                           (k1, ek), (ek,))
                    self.V(lambda e, a=a: e.tensor_tensor(sqq[a], dd[a], dd[a], ALU.mult), (ek,), (ek,))
                    self.V(lambda e, a=a: e.reduce_sum(ssm[a][:, 0:1], sqq[a], mybir.AxisListType.X), (ek,), (ek,))
                    self.A(lambda e, a=a: e.activation(ssm[a][:, 1:2], ssm[a][:, 0:1], AF.Ln, bias=RMS_EPS, scale=1.0 / 128), (ek,), (ek,))
                    self.A(lambda e, a=a: e.activation(ssm[a][:, 2:3], ssm[a][:, 1:2], AF.Exp, scale=-0.5), (ek,), (ek,))
                    self.V(lambda e, a=a: e.scalar_tensor_tensor(ob[a], dd[a], ssm[a][:, 2:3], gsub, ALU.mult, ALU.mult),
                           (ek, "gsub"), ("ob%d" % a,))
                    self.PE([lambda e, a=a, ptr=ptr: e.transpose(ptr[:, a * 128:(a + 1) * 128], ob[a], self.identb)],
                            ("ob%d" % a, "cb"), ("ptr",))
                self.V(lambda e, ei=ei, ptr=ptr: e.tensor_copy(oT[ei], ptr[:, 0:256]), ("ptr",), ("oT%d" % ei,))
                self.ST(self.ATT[h * 128:(h + 1) * 128, q0:q0 + 256], oT[ei], ("oT%d" % ei,), ())

    def layer_norm(self, xin, a, gt, bt, stats, mv, keyx):
        kx = keyx + "%d" % a
        for c in range(4):
            self.V(lambda e, c=c: e.bn_stats(stats[:, c * 6:(c + 1) * 6], xin[:, a, c * 512:(c + 1) * 512]), (kx,), ("stats",))
        self.V(lambda e: e.bn_aggr(mv[:, 0:2], stats[:, 0:24]), ("stats",), ("mv",))
        self.A(lambda e: e.activation(mv[:, 2:3], mv[:, 1:2], AF.Sqrt, bias=LN_EPS, scale=1.0), ("mv",), ("mv",))
        self.V(lambda e: e.reciprocal(mv[:, 3:4], mv[:, 2:3]), ("mv",), ("mv",))
        self.V(lambda e: e.tensor_scalar(xin[:, a, :], xin[:, a, :], mv[:, 0:1], mv[:, 3:4], ALU.subtract, ALU.mult),
               (kx, "mv"), (kx,))
        self.G(lambda e: e.tensor_tensor(xin[:, a, :], xin[:, a, :], gt, ALU.mult), (kx, "lng"), (kx,))
        self.G(lambda e: e.tensor_tensor(xin[:, a, :], xin[:, a, :], bt, ALU.add), (kx, "lnb"), (kx,))

    def phase4(self, l, src, dst):
        S = self.S_tok
        NB = S // 512
        self.carve_reset()
        xin = self.carve([4, D], F32)
        UA = self.carve([64, 512], BF16)
        uaf = UA.rearrange("p a b -> p (a b)")
        GA = uaf[:, 0:8192].rearrange("p (a b) -> p a b", a=16)
        GS = uaf[:, 8192:16384].rearrange("p (a b) -> p a b", a=16)
        AT = uaf[:, 16384:20480].rearrange("p (a b) -> p a b", a=8)
        gtok = uaf[:, 20480:24576].rearrange("p (a b) -> p a b", a=4)
        gT = uaf[:, 24576:28672].rearrange("p (a b) -> p a b", a=8)
        glT = uaf[:, 28672:32768].rearrange("p (a b) -> p a b", a=8)
        hT = UA
        x1b = uaf[:, 0:8192].rearrange("p (a b) -> p a b", a=4)
        MX = self.carve([16, 512], BF16)
        wr = [self.carve([8192], BF16) for _ in range(3)]
        gt = self.carve([D], F32)
        bt = self.carve([D], F32)
        sg = [self.carve([512], F32) for _ in range(2)]
        stats = self.carve([24], F32)
        mv = self.carve([4], F32)
        key = "Wb%d" % l
        def wviews(slot):
            return {"A8": slot.rearrange("p (j k c) -> p j k c", j=8, k=8),
                    "A16": slot.rearrange("p (j k c) -> p j k c", j=4, k=16),
                    "B": slot.rearrange("p (k c) -> p k c", k=16)}
        wlist = [("A8", self.Wglu_b[l][0]), ("A8", self.Wau_b[l][0]), ("A8", self.Wsu_b[l][0]),
                 ("A8", self.Wau_b[l][1]), ("A8", self.Wsu_b[l][1])]
        wlist += [("B", self.Wout_b[l][n]) for n in range(4)]
        wlist += [("A16", self.W1_b[l][g]) for g in range(16)]
        wlist += [("B", self.W2_b[l][n][kg]) for n in range(4) for kg in range(4)]
        NW = len(wlist)
        state = {"i": 0}

        def wload(i):
            if i < NB * NW:
                tag, dv = wlist[i % NW]
                self.LD(wviews(wr[i % 3])[tag], dv, (key,), ("w%d" % (i % 3),))

        def wnext(defer=False):
            i = state["i"]
            state["i"] += 1
            if defer:
                state["defer"] = i + 2
            else:
                wload(i + 2)
            tag, _ = wlist[i % NW]
            return wviews(wr[i % 3])[tag], "w%d" % (i % 3)
        wload(0)
        wload(1)
        bk = {"n": 0}

        def bank():
            b = bk["n"] % 4
            bk["n"] += 1
            return self.ps[b][:], "pm%d" % b
        for t in range(NB):
            t0 = t * 512
            self.LD(xin, src[t0:t0 + 512, :].rearrange("(a p) d -> p a d", p=128), (), ["xr%d" % a for a in range(4)])
            self.LD(GA, self.GAT[:, t0:t0 + 512].rearrange("(k r) t -> r k t", r=128), (), ("GA", "UA"))
            self.LD(GS, self.GST[:, t0:t0 + 512].rearrange("(k r) t -> r k t", r=128), (), ("GS", "UA"))
            self.LD(AT, self.ATT[:, t0:t0 + 512].rearrange("(k r) t -> r k t", r=128), (), ("AT", "UA"))
            self.LD(gtok, self.Gs[t0:t0 + 512, :].rearrange("(a p) c -> p a c", p=128), (), ("gb0", "gb1", "gb2", "gb3", "UA"))
            self.transpose_block(gtok, gT, 8, "g", extra_r=("UA",))
            wt, wk = wnext()
            for jc in range(8):
                pm, pk = bank()
                self.PE([lambda e, kc=kc, jc=jc, wt=wt, pm=pm: e.matmul(pm, wt[:, jc, kc, :], gT[:, kc, :], start=(kc == 0), stop=(kc == 7))
                         for kc in range(8)], (wk, "gT", "UA"), (pk,))
                si = jc % 2
                self.A(lambda e, si=si, pm=pm: e.activation(sg[si], pm, AF.Sigmoid), (pk,), ("sg%d" % si,))
                self.V(lambda e, si=si, jc=jc: e.tensor_tensor(glT[:, jc, :], gT[:, jc, :], sg[si], ALU.mult),
                       ("sg%d" % si, "gT", "UA"), ("glT",))
            for g2 in range(2):
                wa, wak = wnext()
                ws, wsk = wnext(defer=True)
                for jj in range(8):
                    jc = g2 * 8 + jj
                    pa, pak = bank()
                    pS, psk = bank()
                    self.PE([lambda e, kc=kc, jj=jj, wa=wa, pa=pa: e.matmul(pa, wa[:, jj, kc, :], AT[:, kc, :], start=(kc == 0), stop=(kc == 7))
                             for kc in range(8)], (wak, "AT", "UA"), (pak,))
                    self.PE([lambda e, kc=kc, jj=jj, ws=ws, pS=pS: e.matmul(pS, ws[:, jj, kc, :], glT[:, kc, :], start=(kc == 0), stop=(kc == 7))
                             for kc in range(8)], (wsk, "glT", "UA"), (psk,))
                    si = jc % 2
                    self.V(lambda e, si=si, jc=jc, pa=pa: e.tensor_tensor(sg[si], pa, GA[:, jc, :], ALU.mult),
                           (pak, "GA", "UA"), ("sg%d" % si,))
                    self.V(lambda e, si=si, jc=jc, pS=pS: e.tensor_tensor(MX[:, jc, :], pS, GS[:, jc, :], ALU.mult),
                           (psk, "GS", "UA"), ("MX",))
                    self.V(lambda e, si=si, jc=jc: e.tensor_tensor(MX[:, jc, :], MX[:, jc, :], sg[si], ALU.add),
                           ("sg%d" % si, "MX"), ("MX",))
                wload(state["defer"])
            if "DMT" in self.dbg:
                self.ST(self.DMT[t], MX, ("MX",), ())
                self.ST(self.DGL[t], glT, ("glT",), ())
            self.LD(gt, self.ln1_g[l:l + 1, :].partition_broadcast(128), (), ("lng",))
            self.LD(bt, self.ln1_b[l:l + 1, :].partition_broadcast(128), (), ("lnb",))
            for n in range(4):
                wt, wk = wnext()
                for a in range(4):
                    pm, pk = bank()
                    self.PE([lambda e, kc=kc, a=a, wt=wt, pm=pm: e.matmul(pm, MX[:, kc, a * 128:(a + 1) * 128], wt[:, kc, :],
                                                                        start=(kc == 0), stop=(kc == 15))
                             for kc in range(16)], (wk, "MX"), (pk,))
                    self.V(lambda e, a=a, n=n, pm=pm: e.scalar_tensor_tensor(xin[:, a, n * 512:(n + 1) * 512], xin[:, a, n * 512:(n + 1) * 512],
                                                                           ALPHA, pm, ALU.mult, ALU.add), (pk, "xr%d" % a), ("xr%d" % a,))
            for a in range(4):
                self.layer_norm(xin, a, gt, bt, stats, mv, "xr")
                self.A(lambda e, a=a: e.copy(x1b[:, a, :], xin[:, a, :]), ("xr%d" % a,), ("x1b%d" % a, "UA"))
            if "DMT" in self.dbg:
                self.ST(self.DX1[t0:t0 + 512, :].rearrange("(a p) d -> p a d", p=128), xin, ["xr%d" % a for a in range(4)], ())
            self.transpose_block(x1b, MX, 16, "x1", extra_r=("UA",), outkey="MX")
            for g in range(16):
                wt, wk = wnext()
                for jj in range(4):
                    jc = g * 4 + jj
                    pm, pk = bank()
                    self.PE([lambda e, kc=kc, jj=jj, wt=wt, pm=pm: e.matmul(pm, wt[:, jj, kc, :], MX[:, kc, :], start=(kc == 0), stop=(kc == 15))
                             for kc in range(16)], (wk, "MX"), (pk,))
                    si = jc % 2
                    self.A(lambda e, si=si, pm=pm: e.activation(sg[si], pm, AF.Relu), (pk,), ("sg%d" % si,))
                    self.V(lambda e, si=si, jc=jc: e.tensor_tensor(hT[:, jc, :], sg[si], sg[si], ALU.mult), ("sg%d" % si,), ("UA",))
            self.LD(gt, self.ln2_g[l:l + 1, :].partition_broadcast(128), (), ("lng",))
            self.LD(bt, self.ln2_b[l:l + 1, :].partition_broadcast(128), (), ("lnb",))
            for n in range(4):
                for kg in range(4):
                    wt, wk = wnext()
                    for a in range(4):
                        pm = self.ps[a][:]
                        self.PE([lambda e, kc=kc, kg=kg, a=a, wt=wt, pm=pm: e.matmul(pm, hT[:, kg * 16 + kc, a * 128:(a + 1) * 128], wt[:, kc, :],
                                                                                   start=(kg == 0 and kc == 0), stop=(kg == 3 and kc == 15))
                                 for kc in range(16)], (wk, "UA"), ("pm%d" % a,))
                for a in range(4):
                    pm = self.ps[a][:]
                    self.V(lambda e, a=a, n=n, pm=pm: e.scalar_tensor_tensor(xin[:, a, n * 512:(n + 1) * 512], xin[:, a, n * 512:(n + 1) * 512],
                                                                           ALPHA, pm, ALU.mult, ALU.add), ("pm%d" % a, "xr%d" % a), ("xr%d" % a,))
            for a in range(4):
                self.layer_norm(xin, a, gt, bt, stats, mv, "xr")
            self.ST(dst[t0:t0 + 512, :].rearrange("(a p) d -> p a d", p=128), xin, ["xr%d" % a for a in range(4)], ())


def build_inputs(S, NL, x, positions, weights, cst, l0=0):
    m = {"x": np.ascontiguousarray(x, dtype=np.float32),
         "positions": np.ascontiguousarray(positions, dtype=np.int32).reshape(1, S),
         "cst": cst}
    for k, v in weights.items():
        m[k] = np.ascontiguousarray(v[l0:l0 + NL])
    m["lparams"] = np.array([[_lambda_init(l), 1.0 - _lambda_init(l)] for l in range(l0, l0 + NL)], np.float32)
    return m


WEIGHT_NAMES = ["w_in", "lambda_qk", "subln_g", "ssm_a_re", "ssm_a_im", "ssm_log_dt", "ssm_b_re", "ssm_b_im",
                "ssm_c_re", "ssm_c_im", "ssm_d", "w_glu", "w_attn_up", "w_ssm_up", "w_out", "ln1_g", "ln1_b",
                "ln2_g", "ln2_b", "w_mlp_up", "w_mlp_down"]


LAYERS_PER_LAUNCH = 1


def kernel(**inputs):
    x = np.asarray(inputs["x"], dtype=np.float32)
    positions = np.asarray(inputs["positions"])
    weights = {k: np.asarray(inputs[k]) for k in WEIGHT_NAMES}
    cst = host_consts()
    nl = LAYERS_PER_LAUNCH
    b = Builder(SEQ, nl)
    nc = b.build()
    cur = [x[i] for i in range(BATCH)]
    for l0 in range(0, DEPTH, nl):
        in_maps = [build_inputs(SEQ, nl, cur[i], positions[i], weights, cst, l0) for i in range(BATCH)]
        res = run_bass_kernel_spmd(nc, in_maps, core_ids=list(range(BATCH)))
        cur = [np.asarray(r["y"]) for r in res.results]
    return np.stack(cur, axis=0).astype(np.float32)
```
